# Optimizing a Trainium2 kernel written in Bass

```python
import math
import jax, jax.numpy as jnp
from jax import lax
import numpy as np

D_MODEL = 2048
BATCH = 4
SEQ = 4096
DEPTH = 4

CONV_CH = D_MODEL // 2
CONV_K = 3
N_HEADS = 8
HEAD_DIM = 128
ATT_WIDTH = N_HEADS * HEAD_DIM
MOBA_BLOCK = 256
MOBA_TOPK = 3
Q_CHUNK = 32
NUM_BUCKETS = 32
MAX_DISTANCE = 128
D_FF = -(-8 * D_MODEL // (3 * 256)) * 256
EPS = 1e-6
IN_COLS = 3 * CONV_CH + 3 * ATT_WIDTH + 2 * D_MODEL
SPLIT_POINTS = (CONV_CH, 2 * CONV_CH, 3 * CONV_CH,
                3 * CONV_CH + ATT_WIDTH, 3 * CONV_CH + 2 * ATT_WIDTH,
                3 * CONV_CH + 3 * ATT_WIDTH, 3 * CONV_CH + 3 * ATT_WIDTH + D_MODEL)

kernel_name = 'hybrid_shortconv_moba_block'


def rms_norm(x, g):
    xf = x.astype(jnp.float32)
    y = xf * lax.rsqrt(jnp.mean(xf * xf, axis=-1, keepdims=True) + EPS)
    return (y * g.astype(jnp.float32)).astype(x.dtype)


def causal_short_conv(u, w):
    s = u.shape[1]
    up = jnp.pad(u, ((0, 0), (CONV_K - 1, 0), (0, 0)))
    y = w[0] * up[:, 0:s]
    for j in range(1, CONV_K):
        y = y + w[j] * up[:, j:j + s]
    return y


def t5_bucket(dist):
    n = jnp.maximum(dist, 0)
    max_exact = NUM_BUCKETS // 2
    nf = jnp.maximum(n, 1).astype(jnp.float32)
    large = max_exact + (jnp.log(nf / max_exact) / math.log(MAX_DISTANCE / max_exact)
                         * (NUM_BUCKETS - max_exact)).astype(jnp.int32)
    large = jnp.minimum(large, NUM_BUCKETS - 1)
    return jnp.where(n < max_exact, n, large)


def moba_attention(q, k, v, rel_bias):
    b, s, h, dh = q.shape
    nb = max(-(-s // MOBA_BLOCK), MOBA_TOPK)
    sp = nb * MOBA_BLOCK
    q = q.transpose(0, 2, 1, 3)
    k = jnp.pad(k.transpose(0, 2, 1, 3), ((0, 0), (0, 0), (0, sp - s), (0, 0)))
    v = jnp.pad(v.transpose(0, 2, 1, 3), ((0, 0), (0, 0), (0, sp - s), (0, 0)))
    kblk = k.reshape(b, h, nb, MOBA_BLOCK, dh)
    vblk = v.reshape(b, h, nb, MOBA_BLOCK, dh)
    kmean = jnp.mean(kblk.astype(jnp.float32), axis=3)
    table_t = rel_bias.astype(jnp.float32).T
    scale = dh ** -0.5
    b_idx = jnp.arange(b)[:, None, None, None]
    h_idx = jnp.arange(h)[None, :, None, None]
    h_idx5 = jnp.arange(h)[None, :, None, None, None]
    offs = jnp.arange(MOBA_BLOCK)
    blk_ids = jnp.arange(nb)
    sel_slot = jnp.arange(MOBA_TOPK)

    def chunk(c):
        start = c * Q_CHUNK
        blk = start // MOBA_BLOCK
        qc = lax.dynamic_slice_in_dim(q, start, Q_CHUNK, axis=2)
        qpos = start + jnp.arange(Q_CHUNK)
        k_own = lax.dynamic_slice_in_dim(k, blk * MOBA_BLOCK, MOBA_BLOCK, axis=2)
        v_own = lax.dynamic_slice_in_dim(v, blk * MOBA_BLOCK, MOBA_BLOCK, axis=2)
        dist_own = qpos[:, None] - (blk * MOBA_BLOCK + offs)[None, :]
        l_own = (jnp.einsum('bhqd,bhkd->bhqk', qc, k_own, preferred_element_type=jnp.float32) * scale
                 + table_t[:, t5_bucket(dist_own)][None])
        l_own = jnp.where(dist_own >= 0, l_own, -jnp.inf)
        gate = jnp.einsum('bhqd,bhnd->bhqn', qc.astype(jnp.float32), kmean)
        gate = jnp.where(blk_ids < blk, gate, -jnp.inf)
        _, sel = lax.top_k(gate, MOBA_TOPK)
        valid = sel_slot < blk
        k_sel = kblk[b_idx, h_idx, sel]
        v_sel = vblk[b_idx, h_idx, sel]
        dist_sel = qpos[None, None, :, None, None] - (sel[..., None] * MOBA_BLOCK + offs)
        l_sel = (jnp.einsum('bhqd,bhqjkd->bhqjk', qc, k_sel, preferred_element_type=jnp.float32) * scale
                 + table_t[h_idx5, t5_bucket(dist_sel)])
        l_sel = jnp.where(valid[:, None], l_sel, -jnp.inf)
        logits = jnp.concatenate(
            [l_own, l_sel.reshape(b, h, Q_CHUNK, MOBA_TOPK * MOBA_BLOCK)], axis=-1)
        p = jax.nn.softmax(logits, axis=-1).astype(v.dtype)
        p_own = p[..., :MOBA_BLOCK]
        p_sel = p[..., MOBA_BLOCK:].reshape(b, h, Q_CHUNK, MOBA_TOPK, MOBA_BLOCK)
        return (jnp.einsum('bhqk,bhkd->bhqd', p_own, v_own)
                + jnp.einsum('bhqjk,bhqjkd->bhqd', p_sel, v_sel))

    out = lax.map(chunk, jnp.arange(s // Q_CHUNK))
    return out.transpose(1, 0, 3, 2, 4).reshape(b, s, h, dh)


def hybrid_mixer(xn, w_in, conv_w, w_conv_out, w_attn_out, w_mix_out, rel_bias):
    b, s, _ = xn.shape
    proj = xn @ w_in
    h_in, g_b, g_c, q, k, v, gate_conv, gate_att = jnp.split(proj, SPLIT_POINTS, axis=-1)
    y_conv = (g_b * causal_short_conv(g_c * h_in, conv_w)) @ w_conv_out
    att = moba_attention(q.reshape(b, s, N_HEADS, HEAD_DIM),
                         k.reshape(b, s, N_HEADS, HEAD_DIM),
                         v.reshape(b, s, N_HEADS, HEAD_DIM), rel_bias)
    y_att = att.reshape(b, s, ATT_WIDTH) @ w_attn_out
    merged = jax.nn.sigmoid(gate_conv) * y_conv + jax.nn.sigmoid(gate_att) * y_att
    return merged @ w_mix_out


def swiglu(xn, w_gate, w_up, w_down):
    return (jax.nn.silu(xn @ w_gate) * (xn @ w_up)) @ w_down


def setup_inputs(seed: int = 0) -> dict:
    key = jax.random.key(seed)
    ks = jax.random.split(key, 14)

    def normal(k, shape, scale):
        return jax.random.normal(k, shape, jnp.float32) * scale

    return {
        'x': normal(ks[0], (BATCH, SEQ, D_MODEL), 1.0),
        'w_in': normal(ks[1], (DEPTH, D_MODEL, IN_COLS), D_MODEL ** -0.5),
        'conv_w': normal(ks[2], (DEPTH, CONV_K, CONV_CH), CONV_K ** -0.5),
        'w_conv_out': normal(ks[3], (DEPTH, CONV_CH, D_MODEL), CONV_CH ** -0.5),
        'w_attn_out': normal(ks[4], (DEPTH, ATT_WIDTH, D_MODEL), ATT_WIDTH ** -0.5),
        'w_mix_out': normal(ks[5], (DEPTH, D_MODEL, D_MODEL), D_MODEL ** -0.5),
        'rel_bias': normal(ks[6], (NUM_BUCKETS, N_HEADS), 0.5),
        'norm_mix': 1.0 + normal(ks[7], (DEPTH, D_MODEL), 0.02),
        'norm_ffn': 1.0 + normal(ks[8], (DEPTH, D_MODEL), 0.02),
        'w_ffn_gate': normal(ks[9], (DEPTH, D_MODEL, D_FF), D_MODEL ** -0.5),
        'w_ffn_up': normal(ks[10], (DEPTH, D_MODEL, D_FF), D_MODEL ** -0.5),
        'w_ffn_down': normal(ks[11], (DEPTH, D_FF, D_MODEL), D_FF ** -0.5),
        'norm_final': 1.0 + normal(ks[12], (D_MODEL,), 0.02),
    }


def reference(x, w_in, conv_w, w_conv_out, w_attn_out, w_mix_out, rel_bias,
              norm_mix, norm_ffn, w_ffn_gate, w_ffn_up, w_ffn_down, norm_final):
    for l in range(DEPTH):
        x = x + hybrid_mixer(rms_norm(x, norm_mix[l]), w_in[l], conv_w[l],
                             w_conv_out[l], w_attn_out[l], w_mix_out[l], rel_bias)
        x = x + swiglu(rms_norm(x, norm_ffn[l]), w_ffn_gate[l], w_ffn_up[l], w_ffn_down[l])
    return rms_norm(x, norm_final)
```

```python
import numpy as np
import concourse.bass as bass
import concourse.mybir as mybir
from concourse.bass_utils import run_bass_kernel_spmd

F32 = mybir.dt.float32
BF16 = mybir.dt.bfloat16
AF = mybir.ActivationFunctionType
ALU = mybir.AluOpType
AX = mybir.AxisListType

CW = 256
BLK = 256
NEG = -30000.0
EPS = 1e-6


class Cfg:
    def __init__(self, D=2048, NT=2048, L=4, DFF=5632, T=512, KGD=4, NSLOT=6, KG=8):
        self.D, self.NT, self.L, self.DFF, self.T = D, NT, L, DFF, T
        self.CC = D // 2
        self.AW = D // 2
        self.H = self.AW // 128
        self.DC = D // 128
        self.CCH = self.CC // 128
        self.FC = DFF // 128
        self.NB = NT // BLK
        self.NJ = 2 * self.NB
        self.NTILE = NT // T
        self.KGD = KGD
        self.NSLOT = NSLOT
        self.KG = min(KG, self.DC)
        self.groups = [[0, 1], [2, 3], [4, 5], [6, 7]]
        self.stop = None
        self.IN_COLS = 3 * self.CC + 3 * self.AW + 2 * D
        assert self.FC % KGD == 0 and KGD <= self.KG and self.DC % self.KG == 0 and self.NJ >= 8


class Buf:
    __slots__ = ("name", "w", "r", "excl")

    def __init__(self, name, excl=False):
        self.name = name
        self.w = {}
        self.r = {}
        self.excl = excl


ENGSEM = {"pe": "s_pe", "act": "s_act", "dve": "s_dve", "pool": "s_pool"}


class Prog:
    def __init__(self):
        self.streams = {e: [] for e in ("pe", "act", "dve", "pool", "sp")}
        self.cnt = {}
        self.waited = {e: {} for e in self.streams}

    def emit(self, eng, fn, reads=(), writes=(), sem=None, amt=1, inc=True):
        own = ENGSEM.get(eng)
        if sem is None:
            sem = own
        deps = {}
        for b in reads:
            for k, v in b.w.items():
                if deps.get(k, 0) < v:
                    deps[k] = v
            if b.excl:
                for k, v in b.r.items():
                    if k != sem and deps.get(k, 0) < v:
                        deps[k] = v
        for b in writes:
            for k, v in b.w.items():
                if deps.get(k, 0) < v:
                    deps[k] = v
            for k, v in b.r.items():
                if deps.get(k, 0) < v:
                    deps[k] = v
        st = self.streams[eng]
        wd = self.waited[eng]
        for k, v in deps.items():
            if k == own and eng == "pe":
                continue
            if wd.get(k, 0) >= v:
                continue
            st.append(("wait", k, v))
            wd[k] = v
        cur = self.cnt.get(sem, 0)
        if inc:
            val = cur + amt
            self.cnt[sem] = val
            st.append(("op", fn, sem, amt, [b.name for b in reads], [b.name for b in writes]))
        else:
            val = cur + 1
            st.append(("op", fn, None, 0, [b.name for b in reads], [b.name for b in writes]))
        for b in reads:
            if b.r.get(sem, 0) < val:
                b.r[sem] = val
        for b in writes:
            b.w = {sem: val}
            b.r = {}
        return (sem, val)

    def wait_all(self, eng, toks):
        for k, v in toks:
            if self.waited[eng].get(k, 0) < v:
                self.streams[eng].append(("wait", k, v))
                self.waited[eng][k] = v


def C(name, *args, **kwargs):
    return (name, args, kwargs)


def prep_weight(W, kcg):
    L, K, N = W.shape
    nkg = K // (128 * kcg)
    ncg = N // CW
    W = W.reshape(L, nkg, kcg, 128, ncg, CW).transpose(0, 4, 1, 3, 2, 5)
    return np.ascontiguousarray(W).reshape(L, ncg, nkg, 128, kcg * CW)


def t5_bucket_np(n):
    n = np.maximum(n, 0)
    nf = np.maximum(n, 1).astype(np.float32)
    large = 16 + (np.log(nf / 16) / np.log(np.float32(128 / 16)) * 16).astype(np.int32)
    large = np.minimum(large, 31)
    return np.where(n < 16, n, large)


def t5_bucket_ref(n):
    import math
    n = np.maximum(n, 0)
    nf = np.maximum(n, 1).astype(np.float32)
    large = 16 + (np.log(nf / np.float32(16)) / np.float32(math.log(128 / 16)) * np.float32(16)).astype(np.int32)
    large = np.minimum(large, 31)
    return np.where(n < 16, n, large)


def build_program(cfg):
    c = cfg
    D, NT, L, T, H, DC, CCH, FC, NB, NJ = c.D, c.NT, c.L, c.T, c.H, c.DC, c.CCH, c.FC, c.NB, c.NJ
    CC, AW = c.CC, c.AW
    nc = bass.Bass("TRN2", target_bir_lowering=False)
    P = Prog()

    def din(name, shape):
        return nc.dram_tensor(name, list(shape), F32, kind="ExternalInput").ap()

    xT = din("xT", [D, NT])
    w_in = din("w_in", [L, c.IN_COLS // CW, DC // c.KG, 128, c.KG * CW])
    w_co = din("w_co", [L, D // CW, 1, 128, CCH * CW])
    w_ao = din("w_ao", [L, D // CW, 1, 128, H * CW])
    w_mix = din("w_mix", [L, D // CW, DC // c.KG, 128, c.KG * CW])
    w_g = din("w_g", [L, c.DFF // CW, DC // c.KG, 128, c.KG * CW])
    w_u = din("w_u", [L, c.DFF // CW, DC // c.KG, 128, c.KG * CW])
    w_d = din("w_d", [L, D // CW, FC // c.KGD, 128, c.KGD * CW])
    cw_in = din("cw", [128, L * 3 * CCH])
    gm_in = din("gmix", [128, L * DC])
    gf_in = din("gffn", [128, L * DC])
    gl_in = din("gfin", [128, DC])
    bias_in = din("biasd", [128, H * 2 * 128])
    c31_in = din("c31", [128, H])
    vmask_in = din("vmask", [128, NB * NJ])
    flag_in = din("hflag", [128, 1])
    ident_in = din("ident", [128, 128])
    esel_in = din("esel", [128, NJ * 128])
    yT = nc.dram_tensor("yT", [D, NT], F32, kind="ExternalOutput").ap()

    xres = nc.dram_tensor("xres", [D, NT], F32).ap()
    R1 = AW + AW // 2 * 1
    R1 = 2 * AW
    PR = min(512, AW)
    NKP = AW // PR
    HPP = PR // 128
    TPP = PR * NT // AW
    payK = [nc.dram_tensor(f"payK{p}", [PR, NT], BF16) for p in range(NKP)]
    gatK = [nc.dram_tensor(f"gatK{p}", [2 * PR, NT], BF16) for p in range(NKP)]
    payVt = [nc.dram_tensor(f"payV{p}", [PR, NT], BF16) for p in range(NKP)]
    gatVt = [nc.dram_tensor(f"gatV{p}", [2 * PR, NT], BF16) for p in range(NKP)]
    R2 = AW + CC
    pay2 = nc.dram_tensor("pay2", [R2, 8], F32)
    gat2 = nc.dram_tensor("gat2", [2 * R2, 8], F32)
    pay2a, gat2a = pay2.ap(), gat2.ap()
    payVv = [t_.ap().rearrange("a (b c) -> (a b) c", c=AW) for t_ in payVt]
    gatVv = [t_.ap()[0:PR, :].rearrange("a (b c) -> (a b) c", c=AW) for t_ in gatVt]

    sem_names = ["s_pe", "s_act", "s_dve", "s_pool", "d_x", "d_xs", "d_k0", "d_k1", "d_v0", "d_v1",
                 "d_khp0", "d_khp1", "d_kho0", "d_kho1", "d_misc", "d_cc2",
                 "d_out", "d_km", "d_ut", "d_c1", "d_c2", "d_xo0", "d_xo1"] + [f"d_xc{i}" for i in range(4)] + \
                [f"d_w{i}" for i in range(c.NSLOT)] + [f"d_ck{p}" for p in range(NKP)] + \
                [f"d_cv{p}" for p in range(NKP)] + \
                [f"d_vh{a_}{hs_}_{p}" for a_ in "po" for hs_ in range(2) for p in range(NKP)]

    from contextlib import ExitStack
    with ExitStack() as es:
        def sb(name, shape, dt):
            return es.enter_context(nc.sbuf_tensor(name, list(shape), dt))

        sems = {n: es.enter_context(nc.semaphore(n)) for n in sem_names}
        NXC = 4
        xc = [sb(f"xc{i}", [128, T], F32) for i in range(NXC)]
        xo = [sb(f"xo{i}", [128, T], F32) for i in range(2)]
        xn2 = [sb(f"xn{i}", [128, DC, T], BF16) for i in range(2)]
        sq = [sb(f"sq{i}", [128, T], BF16) for i in range(2)]
        rstd = sb("rstd", [128, T], F32)
        slabs = [sb(f"slab{i}", [128, c.KG * CW], BF16) for i in range(c.NSLOT)]
        arena_elems = max(H * NT + CCH * NT, FC * T)
        arena = sb("arena", [128, arena_elems], BF16)
        QT = arena[:, 0:H * NT].rearrange("p (h t) -> p h t", h=H)
        Z = arena[:, H * NT:H * NT + CCH * NT].rearrange("p (h t) -> p h t", h=CCH)
        HH = arena[:, 0:FC * T].rearrange("p (f t) -> p f t", f=FC)
        kv_elems = max(2 * (2 * NT + 2 * NT), DC * T)
        kvar = sb("kvar", [128, kv_elems], BF16)
        KH = [kvar[:, i * 2 * NT:(i + 1) * 2 * NT] for i in range(2)]
        VH = [kvar[:, 4 * NT + i * 2 * NT:4 * NT + (i + 1) * 2 * NT].rearrange("p (c d) -> p c d", d=128)
              for i in range(2)]
        MERG = kvar[:, 0:DC * T].rearrange("p (c t) -> p c t", c=DC)
        ktile = [sb(f"ktile{i}", [128, T], BF16) for i in range(2)]
        vtile = [sb(f"vtile{i}", [128, CW], BF16) for i in range(2)]
        tmpf = [sb(f"tmpf{i}", [128, T + 2], F32) for i in range(4)]
        halo = sb("halo", [128, CCH, 2], F32)
        bfirst = sb("bfirst", [128, CCH, 2], F32)
        utail = sb("utail", [128, CCH, 8], F32)
        uh = sb("uh", [128, CCH, 8], F32)
        kmsum = sb("kmsum", [128, H, 8], F32)
        kmprev = sb("kmprev", [128, H, 8], F32)
        kmT = sb("kmT", [128, H, NJ], BF16)
        fix = [sb(f"fix{i}", [128, CCH], F32) for i in range(3)]
        cwt = sb("cwt", [128, L * 3 * CCH], F32)
        gmt = sb("gmt", [128, L * DC], F32)
        gft = sb("gft", [128, L * DC], F32)
        glt = sb("glt", [128, DC], F32)
        biasd = sb("biasd_sb", [128, H * 2 * 128], F32)
        c31 = sb("c31_sb", [128, H], F32)
        vmask = sb("vmask_sb", [128, NB * NJ], F32)
        hflag = sb("hflag_sb", [128, 1], F32)
        ident = sb("ident_sb", [128, 128], BF16)
        esel = sb("esel_sb", [128, NJ * 128], BF16)
        ones_f = sb("ones_f", [128, 128], BF16)
        ones_1f = sb("ones_1f", [128, 128], F32)
        sacc_sb = [sb(f"sacc_sb{i}", [128, BLK], F32) for i in range(4)]
        ones_b = sb("ones_b", [128, 128], BF16)
        eps_t = sb("eps_t", [128, 1], F32)
        gsb = sb("gsb", [128, 2 * NJ], F32)
        top8 = sb("top8", [128, 16], F32)
        negm = sb("negm", [128, 2 * NJ], BF16)
        negmT = [sb(f"negmT{i}", [128, BLK], BF16) for i in range(2)]
        pt = [sb(f"pt{i}", [128, BLK], BF16) for i in range(4)]
        stmp = [sb(f"stmp{i}", [128, 128], F32) for i in range(2)]
        recip = sb("recip", [128, BLK], F32)
        banks = [es.enter_context(nc.psum_tensor(f"bank{i}", [128, 512], F32)) for i in range(8)]
        block = es.enter_context(nc.Block())

        B = {}

        def buf(name):
            if name not in B:
                B[name] = Buf(name)
            return B[name]

        bankb = [buf(f"bank{i}") for i in range(8)]
        for b_ in bankb:
            b_.excl = True
        slabb = [buf(f"slab{i}") for i in range(c.NSLOT)]
        state = {"slab": 0, "bank": 0, "sq": 0, "tmp": 0, "kt": 0, "vt": 0}

        def next_bank(n_pool=6):
            i = state["bank"]
            state["bank"] = (i + 1) % n_pool
            return i

        def dma(eng, out, in_, reads, writes, sem):
            return P.emit(eng, C("dma_start", out=out, in_=in_), reads, writes, sem=sem, amt=16)

        def load_slab(src, ncols):
            i = state["slab"]
            state["slab"] = (i + 1) % c.NSLOT
            dma("pool", slabs[i][:, 0:ncols], src, [], [slabb[i]], f"d_w{i}")
            return i

        def mm(out, lhsT, rhs, start, stop, reads, writes, inc):
            P.emit("pe", C("matmul", out, lhsT=lhsT, rhs=rhs, start=start, stop=stop),
                   reads, writes, inc=inc)

        for ci_, (dst, src) in enumerate(((cwt, cw_in), (gmt, gm_in), (gft, gf_in), (glt, gl_in), (biasd, bias_in),
                                          (c31, c31_in), (vmask, vmask_in), (hflag, flag_in))):
            dma("sp", dst[:], src, [], [buf(f"c_{ci_}")], "d_misc")
        tokc1 = dma("pool", ident[:], ident_in, [], [buf("c_ident")], "d_c1")
        tokc2 = dma("pool", esel[:], esel_in, [], [buf("c_esel")], "d_c2")
        P.emit("dve", C("memset", ones_f[:], 1.0 / D), [], [buf("ones_f")])
        P.emit("dve", C("memset", ones_1f[:], 1.0), [], [buf("ones_1f")])
        P.emit("dve", C("memset", ones_b[:], 1.0), [], [buf("ones_b")])
        P.emit("dve", C("memset", eps_t[:], EPS), [], [buf("eps_t")])
        P.emit("dve", C("memset", negmT[0][:], 0.0), [], [buf("negmT0")])
        P.emit("dve", C("memset", negmT[1][:], 0.0), [], [buf("negmT1")])
        P.emit("dve", C("memset", halo[:], 0.0), [], [buf("halo")])
        P.emit("dve", C("memset", kmsum[:], 0.0), [], [buf("kmsum")])
        P.emit("dve", C("memset", utail[:], 0.0), [], [buf("utail")])
        for eng in ("pe", "act", "dve"):
            P.wait_all(eng, [("d_misc", P.cnt["d_misc"]), ("d_c1", P.cnt["d_c1"]), ("d_c2", P.cnt["d_c2"])])

        xcb = [buf(f"xc{i}") for i in range(NXC)]
        xob = [buf("xo0"), buf("xo1")]
        xnb2 = [[buf(f"xn{s_}_{dc}") for dc in range(DC)] for s_ in range(2)]
        rstdb = buf("rstd")
        sqb = [buf("sq0"), buf("sq1")]
        tmpb = [buf(f"tmpf{i}") for i in range(4)]
        state["xc"] = 0
        state["xo"] = 0
        state["nslot"] = 0
        norm_ready = {}

        def next_tmp():
            i = state["tmp"]
            state["tmp"] = (i + 1) % 4
            return i

        def xres_buf(i, dc):
            return buf(f"xres{i}_{dc}")

        def load_chunk(src, i, dc):
            r = state["xc"]
            state["xc"] = (r + 1) % NXC
            t0_ = i * T
            dma("sp", xc[r][:], src[dc * 128:(dc + 1) * 128, t0_:t0_ + T], [xres_buf(i, dc)], [xcb[r]], f"d_xc{r}")
            return r

        def store_chunk(dst, i, dc, o, sem=None, wbuf=None):
            t0_ = i * T
            dma("sp", dst[dc * 128:(dc + 1) * 128, t0_:t0_ + T], xo[o][:], [xob[o]],
                [wbuf if wbuf is not None else xres_buf(i, dc)], sem or f"d_xo{o}")

        def norm_stats(src, i):
            nb = 6
            for dc in range(DC):
                r = load_chunk(src, i, dc)
                s_ = state["sq"]
                state["sq"] = 1 - s_
                P.emit("act", C("activation", out=sq[s_][:, 0:T], in_=xc[r][:], func=AF.Square), [xcb[r]], [sqb[s_]])
                mm(banks[nb][:, 0:T], ones_f[:], sq[s_][:, 0:T], dc == 0, dc == DC - 1, [sqb[s_], buf("ones_f")],
                   [bankb[nb]], True)
            P.emit("act", C("activation", out=rstd[:], in_=banks[nb][:, 0:T], func=AF.Sqrt, bias=eps_t[:, 0:1],
                                                 scale=1.0), [bankb[nb], buf("eps_t")], [rstdb])
            P.emit("dve", C("reciprocal", out=rstd[:], in_=rstd[:]), [rstdb], [rstdb])

        def norm_tile(key, src, i, gains, goff):
            if key in norm_ready:
                return
            slot = state["nslot"]
            state["nslot"] = 1 - slot
            norm_stats(src, i)
            for dc in range(DC):
                r = load_chunk(src, i, dc)
                P.emit("dve", C("scalar_tensor_tensor", out=xn2[slot][:, dc, :], in0=xc[r][:],
                                scalar=gains[:, goff + dc:goff + dc + 1], in1=rstd[:], op0=ALU.mult, op1=ALU.mult),
                       [xcb[r], rstdb], [xnb2[slot][dc]])
            norm_ready[key] = slot

        def proj(wd, l, cg_list, nkg, kcg, mov, mov_bufs, evac, n_banks=6):
            for cg in cg_list:
                bk = [next_bank(n_banks), next_bank(n_banks)]
                for kg in range(nkg):
                    si = load_slab(wd[l, cg, kg], kcg * CW)
                    for half in range(2):
                        for kc in range(kcg):
                            first = (kg == 0 and kc == 0)
                            last = (kg == nkg - 1 and kc == kcg - 1)
                            mm(banks[bk[half]][:, 0:T],
                               slabs[si][:, kc * CW + half * 128:kc * CW + half * 128 + 128],
                               mov(kg * kcg + kc), first, last,
                               [slabb[si]] + (mov_bufs(kg * kcg + kc) if callable(mov_bufs) else mov_bufs),
                               [bankb[bk[half]]], kc == kcg - 1)
                for half in range(2):
                    evac(cg * 2 + half, bk[half])

        qtb = [[buf(f"qt{h}_{i}") for i in range(NB)] for h in range(H)]
        zb = [buf(f"z{i}") for i in range(c.NTILE)]
        hb = buf("hh")

        def qt_bufs(h, t0, n):
            return [qtb[h][i] for i in range(t0 // BLK, (t0 + n) // BLK)]

        def norm_key(phase, l_, i_):
            return (phase, l_, i_)

        def norm_args(phase, l_, i_):
            src_ = xT if (l_ == 0 and phase in ("p1", "p3")) else xres
            g_ = gft if phase == "ffn" else gmt
            return (norm_key(phase, l_, i_), src_, i_, g_, l_ * DC)

        def next_norm(phase, l_, i_):
            if i_ + 1 < c.NTILE:
                return (phase, l_, i_ + 1)
            if phase == "p1":
                return ("p3", l_, 0)
            if phase == "p3":
                return ("ffn", l_, 0)
            if l_ + 1 < L:
                return ("p1", l_ + 1, 0)
            return None

        def prefetch_norm(phase, l_, i_):
            nx = next_norm(phase, l_, i_)
            if nx is not None and c.stop is None:
                norm_tile(*norm_args(*nx))

        mergb = buf("merg")
        for l in range(L):
            xsrc = xT if l == 0 else xres
            for i in range(c.NTILE):
                t0 = i * T
                norm_tile(*norm_args("p1", l, i))
                slot = norm_ready[norm_key("p1", l, i)]
                xn = xn2[slot]
                xnb = (lambda kc, slot=slot: [xnb2[slot][kc]])
                if c.stop == "p1a":
                    break
                xmov = lambda kc, xn=xn: xn[:, kc, :]
                qcg0 = 3 * CC // CW
                kcg0 = qcg0 + AW // CW
                vcg0 = kcg0 + AW // CW
                kcg0 = qcg0 + AW // CW

                def evac_k(oc, bk):
                    h = oc - 2 * kcg0
                    s = state["kt"]
                    state["kt"] = 1 - s
                    ktb = buf(f"ktile{s}")
                    P.emit("act", C("activation", out=ktile[s][:], in_=banks[bk][:, 0:T], func=AF.Copy),
                           [bankb[bk]], [ktb])
                    nblk = T // BLK
                    b0 = t0 // BLK
                    P.emit("dve", C("tensor_reduce",
                        out=kmsum[:, h, b0:b0 + nblk], in_=banks[bk][:, 0:T].rearrange("p (b t) -> p b t", t=BLK),
                        axis=AX.X, op=ALU.add), [bankb[bk]], [buf("kmsum")])
                    dma("sp", payK[h // HPP].ap()[(h % HPP) * 128:(h % HPP + 1) * 128, t0:t0 + T], ktile[s][:], [ktb],
                        [buf(f"payk{h}_{i}")],
                        f"d_k{s}")
                proj(w_in, l, list(range(kcg0, kcg0 + AW // CW)), DC // c.KG, c.KG, xmov, xnb, evac_k)
                vcg0 = kcg0 + AW // CW
                for cg in range(AW // CW):
                    vbk = [next_bank() for _ in range(T // 128)]
                    for kg in range(DC // c.KG):
                        si = load_slab(w_in[l, vcg0 + cg, kg], c.KG * CW)
                        for tcn in range(T // 128):
                            for kq in range(c.KG):
                                kc = kg * c.KG + kq
                                mm(banks[vbk[tcn]][:, 0:CW], xn[:, kc, tcn * 128:(tcn + 1) * 128],
                                   slabs[si][:, kq * CW:(kq + 1) * CW], kc == 0, kc == DC - 1,
                                   [slabb[si]] + xnb(kc), [bankb[vbk[tcn]]], kq == c.KG - 1)
                    for tcn in range(T // 128):
                        bk = vbk[tcn]
                        s = state["vt"]
                        state["vt"] = 1 - s
                        vtb = buf(f"vtile{s}")
                        P.emit("act", C("activation", out=vtile[s][:], in_=banks[bk][:, 0:CW],
                                                                          func=AF.Copy), [bankb[bk]], [vtb])
                        r0 = t0 + tcn * 128
                        dma("sp", payVv[r0 // TPP][r0 % TPP:r0 % TPP + 128, cg * CW:(cg + 1) * CW], vtile[s][:], [vtb],
                            [buf(f"payv{cg}_{i}_{tcn}")], f"d_v{s}")
                if i == c.NTILE - 1:
                    payb = [b_ for n_, b_ in B.items() if n_.startswith("payk") or n_.startswith("payv")]
                    for p_ in range(NKP):
                        P.emit("pool", C("collective_compute", "AllGather", ALU.bypass, replica_groups=c.groups,
                                         ins=[payK[p_].ap().opt()], outs=[gatK[p_].ap().opt()]), payb,
                               [buf(f"gatK{p_}")], sem=f"d_ck{p_}", amt=1)
                    for p_ in range(NKP):
                        P.emit("pool", C("collective_compute", "AllGather", ALU.bypass, replica_groups=c.groups,
                                         ins=[payVt[p_].ap().opt()], outs=[gatVt[p_].ap().opt()]), payb,
                               [buf(f"gatV{p_}")], sem=f"d_cv{p_}", amt=1)
                ccg0, hcg0, bcg0 = 2 * CC // CW, 0, CC // CW
                for g2 in range(CC // CW):
                    ctmp = [None, None]

                    def evac_c(oc, bk, ctmp=ctmp):
                        ti = next_tmp()
                        ctmp[oc % 2] = ti
                        P.emit("act", C("activation", out=tmpf[ti][:, 0:T], in_=banks[bk][:, 0:T],
                                                                            func=AF.Copy), [bankb[bk]], [tmpb[ti]])
                    proj(w_in, l, [ccg0 + g2], DC // c.KG, c.KG, xmov, xnb, evac_c)
                    utmp = [None, None]

                    def evac_h(oc, bk, ctmp=ctmp, utmp=utmp):
                        cc = oc - 2 * hcg0
                        ti = next_tmp()
                        utmp[oc % 2] = ti
                        tc_ = ctmp[oc % 2]
                        P.emit("dve", C("tensor_copy", out=tmpf[ti][:, 0:2], in_=halo[:, cc, :]),
                               [buf("halo")], [tmpb[ti]])
                        P.emit("dve", C("tensor_tensor",
                            out=tmpf[ti][:, 2:T + 2], in0=banks[bk][:, 0:T], in1=tmpf[tc_][:, 0:T], op=ALU.mult),
                            [bankb[bk], tmpb[tc_]], [tmpb[ti]])
                        P.emit("dve", C("tensor_copy", out=halo[:, cc, :], in_=tmpf[ti][:, T:T + 2]),
                               [tmpb[ti]], [buf("halo")])
                        if i == c.NTILE - 1:
                            P.emit("dve", C("tensor_copy", out=utail[:, cc, 0:2],
                                                                               in_=tmpf[ti][:, T:T + 2]),
                                   [tmpb[ti]], [buf("utail")])
                        w0 = l * 3 * CCH + 0 * CCH + cc
                        w1 = l * 3 * CCH + 1 * CCH + cc
                        w2 = l * 3 * CCH + 2 * CCH + cc
                        P.emit("dve", C("tensor_scalar",
                            out=tmpf[tc_][:, 0:T], in0=tmpf[ti][:, 0:T], scalar1=cwt[:, w0:w0 + 1], scalar2=None,
                            op0=ALU.mult), [tmpb[ti]], [tmpb[tc_]])
                        P.emit("dve", C("scalar_tensor_tensor",
                            out=tmpf[tc_][:, 0:T], in0=tmpf[ti][:, 1:T + 1], scalar=cwt[:, w1:w1 + 1],
                            in1=tmpf[tc_][:, 0:T], op0=ALU.mult, op1=ALU.add), [tmpb[ti], tmpb[tc_]], [tmpb[tc_]])
                        P.emit("dve", C("scalar_tensor_tensor",
                            out=tmpf[tc_][:, 0:T], in0=tmpf[ti][:, 2:T + 2], scalar=cwt[:, w2:w2 + 1],
                            in1=tmpf[tc_][:, 0:T], op0=ALU.mult, op1=ALU.add), [tmpb[ti], tmpb[tc_]], [tmpb[tc_]])
                    proj(w_in, l, [hcg0 + g2], DC // c.KG, c.KG, xmov, xnb, evac_h)

                    def evac_b(oc, bk, ctmp=ctmp):
                        cc = oc - 2 * bcg0
                        tc_ = ctmp[oc % 2]
                        if i == 0:
                            P.emit("dve", C("tensor_copy", out=bfirst[:, cc, :],
                                                                                in_=banks[bk][:, 0:2]),
                                   [bankb[bk]], [buf("bfirst")])
                        P.emit("dve", C("tensor_tensor",
                            out=Z[:, cc, t0:t0 + T], in0=banks[bk][:, 0:T], in1=tmpf[tc_][:, 0:T], op=ALU.mult),
                            [bankb[bk], tmpb[tc_]], [zb[i]])
                    proj(w_in, l, [bcg0 + g2], DC // c.KG, c.KG, xmov, xnb, evac_b)
                prefetch_norm("p1", l, i)
                qcg0 = 3 * CC // CW

                def evac_q(oc, bk):
                    h = oc - 2 * qcg0
                    P.emit("act", C("activation", out=QT[:, h, t0:t0 + T], in_=banks[bk][:, 0:T],
                                                                      func=AF.Copy), [bankb[bk]], qt_bufs(h, t0, T))
                proj(w_in, l, list(range(qcg0, qcg0 + AW // CW)), DC // c.KG, c.KG, xmov, xnb, evac_q)
            if c.stop is not None and c.stop.startswith("p1"):
                break
            dma("sp", pay2a[0:AW, :].rearrange("(h p) b -> p h b", p=128), kmsum[:], [buf("kmsum")], [buf("pay2k")],
                "d_km")
            dma("sp", pay2a[AW:AW + CC, :].rearrange("(h p) b -> p h b", p=128), utail[:], [buf("utail")],
                [buf("pay2u")], "d_ut")
            P.emit("pool", C("collective_compute",
                "AllGather", ALU.bypass, replica_groups=c.groups,
                ins=[pay2.ap().opt()], outs=[gat2.ap().opt()]), [buf("pay2k"), buf("pay2u")], [buf("gat2")],
                sem="d_cc2", amt=1)
            dma("sp", kmprev[:], gat2a[0:AW, :].rearrange("(h p) b -> p h b", p=128), [buf("gat2")], [buf("kmprev")],
                "d_km")
            dma("sp", uh[:], gat2a[AW:AW + CC, :].rearrange("(h p) b -> p h b", p=128), [buf("gat2")], [buf("uh")],
                "d_ut")
            P.emit("act", C("activation", out=kmT[:, :, 0:NB], in_=kmprev[:, :, 0:NB], func=AF.Copy,
                                                 scale=1.0 / BLK), [buf("kmprev")], [buf("kmT")])
            P.emit("act", C("activation", out=kmT[:, :, NB:NJ], in_=kmsum[:, :, 0:NB], func=AF.Copy,
                                                 scale=1.0 / BLK), [buf("kmsum")], [buf("kmT")])
            w0s = cwt[:, l * 3 * CCH:l * 3 * CCH + CCH]
            w1s = cwt[:, l * 3 * CCH + CCH:l * 3 * CCH + 2 * CCH]
            fb = [buf("fix0"), buf("fix1"), buf("fix2")]
            P.emit("dve", C("tensor_tensor", out=fix[0][:], in0=uh[:, :, 0], in1=w0s, op=ALU.mult),
                   [buf("uh")], [fb[0]])
            P.emit("dve", C("tensor_tensor", out=fix[1][:], in0=uh[:, :, 1], in1=w1s, op=ALU.mult),
                   [buf("uh")], [fb[1]])
            P.emit("dve", C("tensor_tensor", out=fix[0][:], in0=fix[0][:], in1=fix[1][:], op=ALU.add),
                   [fb[0], fb[1]], [fb[0]])
            P.emit("dve", C("scalar_tensor_tensor", out=fix[0][:], in0=fix[0][:], scalar=hflag[:, 0:1],
                                                           in1=bfirst[:, :, 0], op0=ALU.mult, op1=ALU.mult),
                   [fb[0], buf("bfirst")], [fb[0]])
            P.emit("dve", C("tensor_tensor", out=Z[:, :, 0], in0=Z[:, :, 0], in1=fix[0][:], op=ALU.add),
                   [fb[0], zb[0]], [zb[0]])
            P.emit("dve", C("tensor_tensor", out=fix[2][:], in0=uh[:, :, 1], in1=w0s, op=ALU.mult),
                   [buf("uh")], [fb[2]])
            P.emit("dve", C("scalar_tensor_tensor", out=fix[2][:], in0=fix[2][:], scalar=hflag[:, 0:1],
                                                           in1=bfirst[:, :, 1], op0=ALU.mult, op1=ALU.mult),
                   [fb[2], buf("bfirst")], [fb[2]])
            P.emit("dve", C("tensor_tensor", out=Z[:, :, 1], in0=Z[:, :, 1], in1=fix[2][:], op=ALU.add),
                   [fb[2], zb[0]], [zb[0]])
            P.emit("dve", C("memset", halo[:], 0.0), [], [buf("halo")])

            if c.stop == "xchg":
                break
            scale = 128.0 ** -0.5
            gate_slot = {}
            gate_stageA = set()

            def prep_gates(h, lb):
                q0 = lb * BLK
                gps = banks[3][:, 0:2 * NJ]
                gpsb = bankb[3]
                nps = banks[6][0:NJ, 0:256]
                npsb = bankb[6]
                for sidx in range(2):
                    mm(gps[:, sidx * NJ:(sidx + 1) * NJ], QT[:, h, q0 + sidx * 128:q0 + (sidx + 1) * 128],
                       kmT[:, h, :], True, True, [qtb[h][lb], buf("kmT")], [gpsb], True)
                for sidx in range(2):
                    P.emit("dve", C("tensor_tensor",
                        out=gsb[:, sidx * NJ:(sidx + 1) * NJ], in0=gps[:, sidx * NJ:(sidx + 1) * NJ],
                        in1=vmask[:, lb * NJ:(lb + 1) * NJ], op=ALU.add), [gpsb], [buf("gsb")])
                    P.emit("dve", C("max", out=top8[:, sidx * 8:(sidx + 1) * 8],
                                                             in_=gsb[:, sidx * NJ:(sidx + 1) * NJ]),
                           [buf("gsb")], [buf("top8")])
                    P.emit("dve", C("tensor_scalar",
                        out=gsb[:, sidx * NJ:(sidx + 1) * NJ], in0=gsb[:, sidx * NJ:(sidx + 1) * NJ],
                        scalar1=top8[:, sidx * 8 + 2:sidx * 8 + 3], scalar2=NEG, op0=ALU.is_lt, op1=ALU.mult),
                        [buf("gsb"), buf("top8")], [buf("gsb")])
                    P.emit("dve", C("tensor_tensor",
                        out=negm[:, sidx * NJ:(sidx + 1) * NJ], in0=gsb[:, sidx * NJ:(sidx + 1) * NJ],
                        in1=vmask[:, lb * NJ:(lb + 1) * NJ], op=ALU.add), [buf("gsb")], [buf("negm")])
                gate_stageA.add((h, lb))

            def prep_gates_b(h, lb):
                nps = banks[6][0:NJ, 0:256]
                npsb = bankb[6]
                for sidx in range(2):
                    mm(nps[:, sidx * 128:(sidx + 1) * 128], negm[:, sidx * NJ:(sidx + 1) * NJ], ident[:], True, True,
                       [buf("negm")], [npsb], True)
                nt_i = state.get("nt", 0)
                state["nt"] = 1 - nt_i
                ntb = buf(f"negmT{nt_i}")
                P.emit("act", C("activation", out=negmT[nt_i][0:NJ, :], in_=nps, func=AF.Copy),
                       [npsb], [ntb])
                gate_slot[(h, lb)] = nt_i

            att_iters = [(h_, lb_) for h_ in range(H) for lb_ in range(NB)]
            for h in range(H):
                hs = h % 2
                khp, kho = buf(f"khp{hs}"), buf(f"kho{hs}")
                vhp = [buf(f"vhp{hs}_{p_}") for p_ in range(NKP)]
                vho = [buf(f"vho{hs}_{p_}") for p_ in range(NKP)]
                kp_, kr_ = h // HPP, (h % HPP) * 128
                dma("sp", KH[hs][:, 0:NT], gatK[kp_].ap()[kr_:kr_ + 128, :], [buf(f"gatK{kp_}")], [khp], f"d_khp{hs}")
                ownk = [b_ for n_, b_ in B.items() if n_.startswith(f"payk{h}_")]
                dma("sp", KH[hs][:, NT:2 * NT], payK[kp_].ap()[kr_:kr_ + 128, :], ownk, [kho], f"d_kho{hs}")
                ownv = [b_ for n_, b_ in B.items() if n_.startswith("payv")]
                CPP = TPP // 128
                for p_ in range(NKP):
                    dma("sp", VH[hs][:, p_ * CPP:(p_ + 1) * CPP, :],
                        gatVv[p_][:, h * 128:(h + 1) * 128].rearrange("(c p) d -> p c d", p=128),
                        [buf(f"gatV{p_}")], [vhp[p_]], f"d_vhp{hs}_{p_}")
                    dma("sp", VH[hs][:, NT // 128 + p_ * CPP:NT // 128 + (p_ + 1) * CPP, :],
                        payVv[p_][:, h * 128:(h + 1) * 128].rearrange("(c p) d -> p c d", p=128),
                        ownv, [vho[p_]], f"d_vho{hs}_{p_}")
                for lb in range(NB):
                    q0 = lb * BLK
                    jb = NB + lb
                    ab = 0
                    ab = state.get("ab", 0)
                    state["ab"] = 1 - ab
                    acc_bank = 4 + ab
                    oacc = banks[acc_bank][:, 0:BLK]
                    oaccb = bankb[acc_bank]
                    ssb2 = [buf(f"sacc_sb{2 * ab}"), buf(f"sacc_sb{2 * ab + 1}")]
                    sst2 = [sacc_sb[2 * ab], sacc_sb[2 * ab + 1]]
                    sums_ps = banks[3][:, 256:256 + BLK]
                    gps = banks[3][:, 0:2 * NJ]
                    gpsb = bankb[3]
                    nps = banks[6][0:NJ, 0:256]
                    npsb = bankb[6]
                    if (h, lb) not in gate_stageA:
                        prep_gates(h, lb)
                    if (h, lb) not in gate_slot:
                        prep_gates_b(h, lb)
                    nt_i = gate_slot[(h, lb)]
                    ntb = buf(f"negmT{nt_i}")
                    chunks = [(jb, 0), (jb, 1)]
                    for j in range(jb):
                        for cch in range(2):
                            chunks.append((j, cch))
                    nch = len(chunks)
                    sbanks = [0, 1, 2, 7]

                    def qk(ci):
                        j, cch = chunks[ci]
                        sbk = sbanks[ci % 4]
                        kcol = j * BLK + cch * 128
                        own = (j == jb)
                        mm(banks[sbk][:, 0:BLK], KH[hs][:, kcol:kcol + 128], QT[:, h, q0:q0 + BLK], True, own,
                           [khp if j < NB else kho, qtb[h][lb]], [bankb[sbk]], own)
                        if not own:
                            mm(banks[sbk][:, 0:BLK], esel[:, j * 128:(j + 1) * 128], negmT[nt_i][:], False, True,
                               [ntb, buf("c_esel")], [bankb[sbk]], True)

                    def softmax_chunk(ci):
                        j, cch = chunks[ci]
                        sbk = sbanks[ci % 4]
                        pi = ci % 4
                        ptb = buf(f"pt{pi}")
                        own = (j == jb)
                        prevb = (j == jb - 1)
                        S = banks[sbk]
                        d0 = biasd[:, (h * 2 + 0) * 128:(h * 2 + 1) * 128]
                        d1 = biasd[:, (h * 2 + 1) * 128:(h * 2 + 2) * 128]

                        def special(qs, dmat):
                            si_ = state.get("st", 0)
                            state["st"] = 1 - si_
                            stb = buf(f"stmp{si_}")
                            P.emit("dve", C("scalar_tensor_tensor",
                                out=stmp[si_][:], in0=S[:, qs * 128:(qs + 1) * 128], scalar=scale, in1=dmat,
                                op0=ALU.mult, op1=ALU.add), [bankb[sbk]], [stb])
                            P.emit("act", C("activation", out=pt[pi][:, qs * 128:(qs + 1) * 128],
                                                                 in_=stmp[si_][:], func=AF.Exp), [stb], [ptb])

                        def far(lo, hi):
                            P.emit("act", C("activation", out=pt[pi][:, lo:hi], in_=S[:, lo:hi], func=AF.Exp,
                                                                 bias=c31[:, h:h + 1], scale=scale),
                                   [bankb[sbk]], [ptb])
                        if own and cch == 0:
                            special(0, d0)
                            special(1, d1)
                        elif own and cch == 1:
                            P.emit("dve", C("memset", pt[pi][:, 0:128], 0.0), [], [ptb])
                            special(1, d0)
                        elif prevb and cch == 1:
                            special(0, d1)
                            far(128, 256)
                        else:
                            far(0, 256)

                    def pv(ci):
                        j, cch = chunks[ci]
                        pi = ci % 4
                        ptb = buf(f"pt{pi}")
                        vc = j * 2 + cch
                        mm(oacc, VH[hs][:, vc, :], pt[pi][:], ci == 0, ci == nch - 1, [(vhp + vho)[vc // CPP], ptb],
                           [oaccb], True)
                        se = ci % 2
                        seng = "pool" if se == 0 else "dve"
                        if ci < 2:
                            P.emit(seng, C("tensor_copy", out=sst2[se][:], in_=pt[pi][:]), [ptb], [ssb2[se]])
                        else:
                            P.emit(seng, C("tensor_tensor", out=sst2[se][:], in0=sst2[se][:], in1=pt[pi][:],
                                           op=ALU.add), [ptb, ssb2[se]], [ssb2[se]])

                    LOOK = 3
                    for ci in range(min(LOOK, nch)):
                        qk(ci)
                    it_n = att_iters.index((h, lb))
                    fin_prev = state.pop("fin", None)
                    for ci in range(nch):
                        softmax_chunk(ci)
                        if ci + LOOK < nch:
                            qk(ci + LOOK)
                        pv(ci)
                        if ci == 3 and fin_prev is not None:
                            fin_prev()
                        if ci == 1 and it_n + 1 < len(att_iters):
                            prep_gates(*att_iters[it_n + 1])
                        if ci == 10 and it_n + 1 < len(att_iters):
                            prep_gates_b(*att_iters[it_n + 1])
                    def fin(h=h, lb=lb, q0=q0, oacc=oacc, oaccb=oaccb, sst2=sst2, ssb2=ssb2, sums_ps=sums_ps):
                        mm(sums_ps, ones_1f[:], sst2[0][:], True, False, [ssb2[0], buf("ones_1f")], [bankb[3]], False)
                        mm(sums_ps, ones_1f[:], sst2[1][:], False, True, [ssb2[1], buf("ones_1f")], [bankb[3]], True)
                        P.emit("dve", C("reciprocal", out=recip[:], in_=sums_ps), [bankb[3]], [buf("recip")])
                        P.emit("dve", C("tensor_tensor", out=QT[:, h, q0:q0 + BLK], in0=oacc, in1=recip[:], op=ALU.mult),
                               [oaccb, buf("recip")], [qtb[h][lb]])
                    state["fin"] = fin
            if state.get("fin") is not None:
                state.pop("fin")()

            if c.stop == "att":
                break
            kvall = [buf(f"{n_}{i_}") for n_ in ("khp", "kho") for i_ in range(2)] + \
                    [buf(f"{n_}{i_}_{p_}") for n_ in ("vhp", "vho") for i_ in range(2) for p_ in range(NKP)]
            mwr = [mergb] + kvall
            for i in range(c.NTILE):
                t0 = i * T
                norm_tile(*norm_args("p3", l, i))
                slot = norm_ready[norm_key("p3", l, i)]
                xn = xn2[slot]
                xnb = (lambda kc, slot=slot: [xnb2[slot][kc]])
                xmov = lambda kc, xn=xn: xn[:, kc, :]
                gccg0 = (3 * CC + 3 * AW) // CW
                gacg0 = gccg0 + D // CW
                qall = [b_ for hh in range(H) for b_ in qt_bufs(hh, t0, T)]
                for g2 in range(D // CW):
                    if g2 == D // CW // 2:
                        prefetch_norm("p3", l, i)
                    sg = {}

                    def evac_sig(tag):
                        def f(oc, bk, tag=tag):
                            ti = next_tmp()
                            sg[(tag, oc % 2)] = ti
                            P.emit("act", C("activation", out=tmpf[ti][:, 0:T],
                                                                                in_=banks[bk][:, 0:T], func=AF.Sigmoid),
                                   [bankb[bk]], [tmpb[ti]])
                        return f
                    proj(w_in, l, [gccg0 + g2], DC // c.KG, c.KG, xmov, xnb, evac_sig("gc"))
                    proj(w_in, l, [gacg0 + g2], DC // c.KG, c.KG, xmov, xnb, evac_sig("ga"))

                    def evac_yc(oc, bk):
                        t1 = sg[("gc", oc % 2)]
                        P.emit("dve", C("tensor_tensor", out=tmpf[t1][:, 0:T], in0=banks[bk][:, 0:T],
                                                                               in1=tmpf[t1][:, 0:T], op=ALU.mult),
                               [bankb[bk], tmpb[t1]], [tmpb[t1]])
                    proj(w_co, l, [g2], 1, CCH, lambda kc: Z[:, kc, t0:t0 + T], [zb[i]], evac_yc)

                    def evac_ya(oc, bk):
                        t1, t2 = sg[("gc", oc % 2)], sg[("ga", oc % 2)]
                        P.emit("dve", C("tensor_tensor", out=tmpf[t2][:, 0:T], in0=banks[bk][:, 0:T],
                                                                               in1=tmpf[t2][:, 0:T], op=ALU.mult),
                               [bankb[bk], tmpb[t2]], [tmpb[t2]])
                        P.emit("dve", C("tensor_tensor",
                            out=MERG[:, oc, :], in0=tmpf[t1][:, 0:T], in1=tmpf[t2][:, 0:T], op=ALU.add),
                            [tmpb[t1], tmpb[t2]], mwr)
                    proj(w_ao, l, [g2], 1, H, lambda kc: QT[:, kc, t0:t0 + T], qall, evac_ya)

                def make_evac_res(src_, i_):
                    def evac_res(oc, bk):
                        r = load_chunk(src_, i_, oc)
                        o = state["xo"]
                        state["xo"] = 1 - o
                        P.emit("dve", C("tensor_tensor", out=xo[o][:], in0=banks[bk][:, 0:T], in1=xc[r][:], op=ALU.add),
                               [bankb[bk], xcb[r]], [xob[o]])
                        store_chunk(xres, i_, oc, o)
                    return evac_res
                proj(w_mix, l, list(range(D // CW)), DC // c.KG, c.KG, lambda kc: MERG[:, kc, :], mwr, make_evac_res(xsrc, i))
            if c.stop == "p3":
                break
            harena = [hb] + [b_ for hh in range(H) for b_ in qtb[hh]] + zb
            for i in range(c.NTILE):
                t0 = i * T
                norm_tile(*norm_args("ffn", l, i))
                slot = norm_ready[norm_key("ffn", l, i)]
                xn = xn2[slot]
                xnb = (lambda kc, slot=slot: [xnb2[slot][kc]])
                xmov = lambda kc, xn=xn: xn[:, kc, :]
                for g2 in range(c.DFF // CW):
                    if g2 == c.DFF // CW // 2:
                        prefetch_norm("ffn", l, i)
                    sl = {}

                    def evac_silu(oc, bk):
                        ti = next_tmp()
                        sl[oc % 2] = ti
                        P.emit("act", C("activation", out=tmpf[ti][:, 0:T], in_=banks[bk][:, 0:T],
                                                                            func=AF.Silu), [bankb[bk]], [tmpb[ti]])
                    proj(w_g, l, [g2], DC // c.KG, c.KG, xmov, xnb, evac_silu)

                    def evac_up(oc, bk):
                        ti = sl[oc % 2]
                        P.emit("dve", C("tensor_tensor",
                            out=HH[:, oc, :], in0=banks[bk][:, 0:T], in1=tmpf[ti][:, 0:T], op=ALU.mult),
                            [bankb[bk], tmpb[ti]], harena if oc == 0 else [hb])
                    proj(w_u, l, [g2], DC // c.KG, c.KG, xmov, xnb, evac_up)
                proj(w_d, l, list(range(D // CW)), FC // c.KGD, c.KGD, lambda kc: HH[:, kc, :], [hb],
                     make_evac_res(xres, i))
            P.emit("pe", C("matmul", banks[7][:, 0:8], lhsT=ones_b[:], rhs=ones_b[:, 0:8], start=True, stop=True),
                   harena + [buf("ones_b")], [bankb[7]])
            for b_ in harena[1:]:
                b_.w = dict(hb.w)
                b_.r = dict(hb.r)
        if c.stop is None:
            for i in range(c.NTILE):
                norm_stats(xres, i)
                for dc in range(DC):
                    r = load_chunk(xres, i, dc)
                    o = state["xo"]
                    state["xo"] = 1 - o
                    P.emit("dve", C("scalar_tensor_tensor", out=xo[o][:], in0=xc[r][:], scalar=glt[:, dc:dc + 1],
                                    in1=rstd[:], op0=ALU.mult, op1=ALU.mult), [xcb[r], rstdb], [xob[o]])
                    store_chunk(yT, i, dc, o, wbuf=buf(f"yout{i}_{dc}"))
        P.wait_all("sp", [(k_, v_) for k_, v_ in P.cnt.items() if k_.startswith("d_") and k_[2] not in "wc"])
        P.wait_all("pool", [(k_, v_) for k_, v_ in P.cnt.items() if k_.startswith("d_") and k_[2] in "wc"])
        for eng_ in ("act", "dve", "pe"):
            P.wait_all("sp", [(ENGSEM[eng_], P.cnt.get(ENGSEM[eng_], 0))])

        def run(e, stream):
            for it in stream:
                if it[0] == "wait":
                    e.wait_ge(sems[it[1]], it[2])
                else:
                    name_, args_, kw_ = it[1]
                    ins = getattr(e, name_)(*args_, **kw_)
                    if it[2] is not None:
                        if it[2] in ("d_cc1", "d_cc2"):
                            ins.then_inc(sems[it[2]])
                        else:
                            ins.then_inc(sems[it[2]], it[3])

        @block.sync
        def _(e):
            run(e, P.streams["sp"])

        @block.gpsimd
        def _(e):
            run(e, P.streams["pool"])

        @block.scalar
        def _(e):
            run(e, P.streams["act"])

        @block.vector
        def _(e):
            run(e, P.streams["dve"])

        @block.tensor
        def _(e):
            run(e, P.streams["pe"])
    nc._dbg_streams = P.streams
    return nc


def make_inputs(cfg, x, w_in, conv_w, w_conv_out, w_attn_out, w_mix_out, rel_bias, norm_mix, norm_ffn,
                w_ffn_gate, w_ffn_up, w_ffn_down, norm_final):
    c = cfg
    f = lambda a: np.ascontiguousarray(np.asarray(a, dtype=np.float32))
    L, H, NJ, NB = c.L, c.H, c.NJ, c.NB
    shared = {
        "w_in": prep_weight(f(w_in), c.KG),
        "w_co": prep_weight(f(w_conv_out), c.CCH),
        "w_ao": prep_weight(f(w_attn_out), c.H),
        "w_mix": prep_weight(f(w_mix_out), c.KG),
        "w_g": prep_weight(f(w_ffn_gate), c.KG),
        "w_u": prep_weight(f(w_ffn_up), c.KG),
        "w_d": prep_weight(f(w_ffn_down), c.KGD),
        "cw": f(f(conv_w).reshape(L, 3, c.CCH, 128).transpose(3, 0, 1, 2).reshape(128, -1)),
        "gmix": f(f(norm_mix).reshape(L, c.DC, 128).transpose(2, 0, 1).reshape(128, -1)),
        "gffn": f(f(norm_ffn).reshape(L, c.DC, 128).transpose(2, 0, 1).reshape(128, -1)),
        "gfin": f(f(norm_final).reshape(c.DC, 128).T),
        "ident": np.eye(128, dtype=np.float32),
    }
    rb = f(rel_bias)
    kk = np.arange(128)[:, None]
    qq = np.arange(128)[None, :]
    idx0 = t5_bucket_ref(qq - kk)
    idx1 = t5_bucket_ref(128 + qq - kk)
    bd = np.zeros((128, H, 2, 128), np.float32)
    for h in range(H):
        d0 = rb[idx0, h]
        d0 = np.where(qq - kk >= 0, d0, np.float32(NEG))
        bd[:, h, 0, :] = d0
        bd[:, h, 1, :] = rb[idx1, h]
    shared["biasd"] = f(bd.reshape(128, -1))
    shared["c31"] = f(np.broadcast_to(rb[31][None, :], (128, H)))
    es = np.zeros((128, NJ, 128), np.float32)
    for j in range(NJ):
        es[j, j, :] = 1.0
    shared["esel"] = f(es.reshape(128, NJ * 128))
    maps = []
    xx = f(x)
    for core in range(8):
        b, half = core // 2, core % 2
        m = dict(shared)
        m["xT"] = f(xx[b, half * c.NT:(half + 1) * c.NT, :].T)
        vm = np.zeros((NB, NJ), np.float32)
        for lb in range(NB):
            for j in range(NJ):
                if j >= NB + lb:
                    vm[lb, j] = NEG
                elif j < NB and half == 0:
                    vm[lb, j] = NEG
        m["vmask"] = f(np.broadcast_to(vm.reshape(1, -1), (128, NB * NJ)))
        m["hflag"] = np.full((128, 1), float(half), np.float32)
        maps.append(m)
    return maps


def run_cfg(cfg, inputs, B=4):
    nc = build_program(cfg)
    maps = make_inputs(cfg, **inputs)
    res = run_bass_kernel_spmd(nc, maps, core_ids=list(range(8)))
    S = 2 * cfg.NT
    out = np.empty((B, S, cfg.D), np.float32)
    for core in range(8):
        b, half = core // 2, core % 2
        out[b, half * cfg.NT:(half + 1) * cfg.NT, :] = np.asarray(res.results[core]["yT"]).T
    return out


def kernel(x, w_in, conv_w, w_conv_out, w_attn_out, w_mix_out, rel_bias, norm_mix, norm_ffn,
           w_ffn_gate, w_ffn_up, w_ffn_down, norm_final):
    cfg = Cfg()
    return run_cfg(cfg, dict(x=x, w_in=w_in, conv_w=conv_w, w_conv_out=w_conv_out, w_attn_out=w_attn_out,
                             w_mix_out=w_mix_out, rel_bias=rel_bias, norm_mix=norm_mix, norm_ffn=norm_ffn,
                             w_ffn_gate=w_ffn_gate, w_ffn_up=w_ffn_up, w_ffn_down=w_ffn_down,
                             norm_final=norm_final))
```
